# Optimizing a Trainium2 kernel written in Bass

```python
import jax, jax.numpy as jnp
from jax import lax
import numpy as np

D_MODEL = 1024
BATCH = 16
SEQ = 4096
DEPTH = 2

POOL_WINDOWS = (2, 4, 8, 16)
POOL_GROUPS = len(POOL_WINDOWS)
POOL_WIDTH = D_MODEL // 2
POOL_GC = POOL_WIDTH // POOL_GROUPS
CONV_WIDTH = D_MODEL // 2
CONV_HEADS = 8
CONV_K = 3
EVEN_IN = POOL_WIDTH + 3 * CONV_WIDTH
EVEN_MIX = POOL_WIDTH + CONV_WIDTH
SGU_CHUNK = 128
SGU_WIDTH = 2 * D_MODEL
SGU_HEADS = 8
SGU_HD = SGU_WIDTH // SGU_HEADS
D_FF = 2816
FFN_K = 3
PLE_DIM = 256
N_EVEN = (DEPTH + 1) // 2
N_ODD = DEPTH // 2
DN_ALPHA = (2 * DEPTH) ** 0.25
DN_BETA = (8 * DEPTH) ** -0.25
LN_EPS = 1e-5

kernel_name = "hybrid_pool_shortconv_gmlp_deepnorm"


def layer_norm(x, g, b):
    xf = x.astype(jnp.float32)
    mu = jnp.mean(xf, axis=-1, keepdims=True)
    var = jnp.mean(jnp.square(xf - mu), axis=-1, keepdims=True)
    y = (xf - mu) * lax.rsqrt(var + LN_EPS)
    return (y * g.astype(jnp.float32) + b.astype(jnp.float32)).astype(x.dtype)


def causal_dwconv(h, w):
    K = w.shape[0]
    S = h.shape[1]
    hp = jnp.pad(h, ((0, 0), (K - 1, 0), (0, 0)))
    y = hp[:, 0:S] * w[0]
    for k in range(1, K):
        y = y + hp[:, k:k + S] * w[k]
    return y


def pool_mixer(h, mix, scale):
    Bn, S, _ = h.shape
    hg = h.reshape(Bn, S, POOL_GROUPS, POOL_GC).astype(jnp.float32)
    cs = jnp.cumsum(hg, axis=1)
    lagged = jnp.stack(
        [jnp.pad(cs[:, :S - w, g], ((0, 0), (w, 0), (0, 0)))
         for g, w in enumerate(POOL_WINDOWS)], axis=2)
    cnt = jnp.minimum(jnp.arange(1, S + 1, dtype=jnp.float32)[:, None],
                      jnp.array(POOL_WINDOWS, dtype=jnp.float32)[None, :])
    mean = (cs - lagged) / cnt[None, :, :, None]
    d = (mean - hg).astype(h.dtype)
    y = jnp.einsum('bsgc,gcd->bsgd', d, mix).reshape(Bn, S, POOL_WIDTH)
    return y * scale


def even_mixer(x, w_in, pool_mix, pool_scale, conv_w, w_out):
    z = x @ w_in
    h_a = z[..., :POOL_WIDTH]
    b_gate = z[..., POOL_WIDTH:POOL_WIDTH + CONV_WIDTH]
    c_gate = z[..., POOL_WIDTH + CONV_WIDTH:POOL_WIDTH + 2 * CONV_WIDTH]
    val = z[..., POOL_WIDTH + 2 * CONV_WIDTH:]
    y_a = pool_mixer(h_a, pool_mix, pool_scale)
    y_b = b_gate * causal_dwconv(c_gate * val, conv_w)
    return jnp.concatenate([y_a, y_b], axis=-1) @ w_out


def odd_mixer(x, w_in, ln_g, ln_b, sgu_w, sgu_b, w_out):
    Bn, S, _ = x.shape
    z = jax.nn.gelu(x @ w_in, approximate=False)
    u = z[..., :SGU_WIDTH]
    v = layer_norm(z[..., SGU_WIDTH:], ln_g, ln_b)
    v = v.reshape(Bn, S // SGU_CHUNK, SGU_CHUNK, SGU_HEADS, SGU_HD)
    causal = jnp.tril(jnp.ones((SGU_CHUNK, SGU_CHUNK), dtype=sgu_w.dtype))
    w_m = sgu_w * causal
    mixed = jnp.einsum('hts,bnshd->bnthd', w_m, v) + sgu_b.T[:, :, None]
    mixed = mixed.reshape(Bn, S, SGU_WIDTH)
    return (u * mixed) @ w_out


def conv_ffn(x, w_up, conv_w, conv_b, w_down):
    z = x @ w_up
    a = causal_dwconv(z[..., :D_FF], conv_w) + conv_b
    h = jax.nn.gelu(a, approximate=False) * z[..., D_FF:]
    return h @ w_down


def setup_inputs(seed: int = 0) -> dict:
    key = jax.random.key(seed)
    ks = jax.random.split(key, 32)
    f32 = jnp.float32
    nrm = lambda k, shape, s: jax.random.normal(k, shape, f32) * s
    return {
        "x": nrm(ks[0], (BATCH, SEQ, D_MODEL), 1.0),
        "p": nrm(ks[1], (DEPTH, BATCH, SEQ, PLE_DIM), 1.0),
        "even_w_in": nrm(ks[2], (N_EVEN, D_MODEL, EVEN_IN), D_MODEL ** -0.5),
        "even_pool_mix": nrm(ks[3], (N_EVEN, POOL_GROUPS, POOL_GC, POOL_GC), POOL_GC ** -0.5),
        "even_pool_scale": 1.0 + nrm(ks[4], (N_EVEN, POOL_WIDTH), 0.1),
        "even_conv_w": nrm(ks[5], (N_EVEN, CONV_K, CONV_WIDTH), CONV_K ** -0.5),
        "even_w_out": nrm(ks[6], (N_EVEN, EVEN_MIX, D_MODEL), EVEN_MIX ** -0.5 * DN_BETA),
        "odd_w_in": nrm(ks[7], (N_ODD, D_MODEL, 2 * SGU_WIDTH), D_MODEL ** -0.5),
        "odd_sgu_ln_g": 1.0 + nrm(ks[8], (N_ODD, SGU_WIDTH), 0.1),
        "odd_sgu_ln_b": nrm(ks[9], (N_ODD, SGU_WIDTH), 0.02),
        "odd_sgu_w": nrm(ks[10], (N_ODD, SGU_HEADS, SGU_CHUNK, SGU_CHUNK), SGU_CHUNK ** -0.5),
        "odd_sgu_b": 1.0 + nrm(ks[11], (N_ODD, SGU_HEADS, SGU_CHUNK), 0.1),
        "odd_w_out": nrm(ks[12], (N_ODD, SGU_WIDTH, D_MODEL), SGU_WIDTH ** -0.5 * DN_BETA),
        "mix_ln_g": 1.0 + nrm(ks[13], (DEPTH, D_MODEL), 0.1),
        "mix_ln_b": nrm(ks[14], (DEPTH, D_MODEL), 0.02),
        "ffn_w_up": nrm(ks[15], (DEPTH, D_MODEL, 2 * D_FF), D_MODEL ** -0.5),
        "ffn_conv_w": nrm(ks[16], (DEPTH, FFN_K, D_FF), FFN_K ** -0.5),
        "ffn_conv_b": nrm(ks[17], (DEPTH, D_FF), 0.02),
        "ffn_w_down": nrm(ks[18], (DEPTH, D_FF, D_MODEL), D_FF ** -0.5 * DN_BETA),
        "ffn_ln_g": 1.0 + nrm(ks[19], (DEPTH, D_MODEL), 0.1),
        "ffn_ln_b": nrm(ks[20], (DEPTH, D_MODEL), 0.02),
        "ple_w_gate": nrm(ks[21], (DEPTH, D_MODEL, D_MODEL), D_MODEL ** -0.5),
        "ple_b_gate": nrm(ks[22], (DEPTH, D_MODEL), 0.02),
        "ple_w_proj": nrm(ks[23], (DEPTH, PLE_DIM, D_MODEL), PLE_DIM ** -0.5),
    }


def reference(x, p, even_w_in, even_pool_mix, even_pool_scale, even_conv_w, even_w_out,
              odd_w_in, odd_sgu_ln_g, odd_sgu_ln_b, odd_sgu_w, odd_sgu_b, odd_w_out,
              mix_ln_g, mix_ln_b, ffn_w_up, ffn_conv_w, ffn_conv_b, ffn_w_down,
              ffn_ln_g, ffn_ln_b, ple_w_gate, ple_b_gate, ple_w_proj):
    for i in range(DEPTH):
        j = i // 2
        if i % 2 == 0:
            mixed = even_mixer(x, even_w_in[j], even_pool_mix[j], even_pool_scale[j],
                               even_conv_w[j], even_w_out[j])
        else:
            mixed = odd_mixer(x, odd_w_in[j], odd_sgu_ln_g[j], odd_sgu_ln_b[j],
                              odd_sgu_w[j], odd_sgu_b[j], odd_w_out[j])
        x = layer_norm(DN_ALPHA * x + mixed, mix_ln_g[i], mix_ln_b[i])
        ff = conv_ffn(x, ffn_w_up[i], ffn_conv_w[i], ffn_conv_b[i], ffn_w_down[i])
        x = layer_norm(DN_ALPHA * x + ff, ffn_ln_g[i], ffn_ln_b[i])
        gate = jax.nn.sigmoid(x @ ple_w_gate[i] + ple_b_gate[i])
        x = x + gate * (p[i] @ ple_w_proj[i])
    return x
```

```python
import numpy as np
from contextlib import ExitStack
import concourse.bass as bass
import concourse.mybir as mybir
from concourse.bass_utils import run_bass_kernel_spmd

F32 = mybir.dt.float32
BF16 = mybir.dt.bfloat16
ALU = mybir.AluOpType
AF = mybir.ActivationFunctionType

D = 1024
SEQ = 4096
BATCH = 16
NCORES = 8
TT = 512
DFF = 2816
NFF = DFF // 128
PLE = 256
ALPHA = float((2 * 2) ** 0.25)
EPS = 1e-5
POOL_W = (2, 4, 8, 16)
NSLOT = 6
SLOTC = 4096
NT32 = 10


def _kxn(W, cols):
    K = W.shape[0]
    sub = np.asarray(W)[:, cols]
    return np.ascontiguousarray(sub.reshape(K // 128, 128, -1).transpose(1, 0, 2).reshape(128, -1))


def _piece_list(layers):
    out = []
    for l in layers:
        if l == 0:
            out += [("in0", 4096)] * 4 + [("pmix", 512)] + [("out0", 4096)] * 2
        else:
            out += [("v", 4096)] * 4 + [("u", 4096)] * 4 + [("out1", 4096)] * 4
        out += [("up", 4096)] * 11 + [("down", NFF * 128)] * 8
        out += [("gate", 4096), ("proj", 2048), ("gate", 4096)]
    return out


def _build_wstream(inp, layers):
    ar = np.arange
    pcs = []
    for l in layers:
        if l == 0:
            W = inp["even_w_in"][0]
            for c0 in (0, 1024, 1536, 512):
                pcs.append(_kxn(W, ar(c0, c0 + 512)))
            pm = np.asarray(inp["even_pool_mix"][0])
            pcs.append(np.ascontiguousarray(pm.transpose(1, 0, 2).reshape(128, 512)))
            W = inp["even_w_out"][0]
            for q in range(2):
                pcs.append(_kxn(W, ar(q * 512, (q + 1) * 512)))
        else:
            W = inp["odd_w_in"][0]
            for q in range(4):
                pcs.append(_kxn(W, ar(2048 + q * 512, 2048 + (q + 1) * 512)))
            for q in range(4):
                pcs.append(_kxn(W, ar(q * 512, (q + 1) * 512)))
            W = inp["odd_w_out"][0]
            for q in range(4):
                pcs.append(_kxn(W, ar(q * 256, (q + 1) * 256)))
        W = inp["ffn_w_up"][l]
        for j in range(11):
            cols = np.concatenate([ar(2 * j * 128, (2 * j + 2) * 128), DFF + ar(2 * j * 128, (2 * j + 2) * 128)])
            pcs.append(_kxn(W, cols))
        W = inp["ffn_w_down"][l]
        for m in range(8):
            pcs.append(_kxn(W, ar(m * 128, (m + 1) * 128)))
        Wg = inp["ple_w_gate"][l]
        Wp = inp["ple_w_proj"][l]
        pcs.append(_kxn(Wg, ar(0, 512)))
        pcs.append(_kxn(Wp, ar(0, 1024)))
        pcs.append(_kxn(Wg, ar(512, 1024)))
    ws = np.zeros((len(pcs), 128, SLOTC), np.float32)
    for i, a in enumerate(pcs):
        ws[i, :, : a.shape[1]] = a
    return ws


def _cst_layout():
    off = {}
    n = 0

    def add(name, w):
        nonlocal n
        off[name] = n
        n += w
    for l in (0, 1):
        for nm, w in (("mg", 8), ("mb", 8), ("fg", 8), ("fb", 8), ("fcw", 66), ("fcb", 22), ("pb", 8)):
            add(f"{nm}{l}", w)
    add("psc", 4)
    add("cw", 12)
    add("sg", 16)
    add("sb", 16)
    add("invcnt", 64)
    add("mask", 128)
    add("swT", 1024)
    add("sbb", 1024)
    return off, n


def _vcols(v):
    v = np.asarray(v, np.float32)
    return v.reshape(-1, 128).T


def _build_cst(inp):
    off, n = _cst_layout()
    c = np.zeros((128, n), np.float32)

    def put(name, arr):
        c[:, off[name]: off[name] + arr.shape[1]] = arr
    for l in (0, 1):
        put(f"mg{l}", _vcols(inp["mix_ln_g"][l]))
        put(f"mb{l}", _vcols(inp["mix_ln_b"][l]))
        put(f"fg{l}", _vcols(inp["ffn_ln_g"][l]))
        put(f"fb{l}", _vcols(inp["ffn_ln_b"][l]))
        cw = np.asarray(inp["ffn_conv_w"][l])
        put(f"fcw{l}", np.concatenate([_vcols(cw[k]) for k in range(3)], 1))
        put(f"fcb{l}", _vcols(inp["ffn_conv_b"][l]))
        put(f"pb{l}", _vcols(inp["ple_b_gate"][l]))
    put("psc", _vcols(inp["even_pool_scale"][0]))
    cw = np.asarray(inp["even_conv_w"][0])
    put("cw", np.concatenate([_vcols(cw[k]) for k in range(3)], 1))
    put("sg", _vcols(inp["odd_sgu_ln_g"][0]))
    put("sb", _vcols(inp["odd_sgu_ln_b"][0]))
    t = np.arange(16, dtype=np.float32)
    ic = np.stack([1.0 / np.minimum(t + 1.0, float(w)) for w in POOL_W], 0).astype(np.float32)
    put("invcnt", np.broadcast_to(ic.reshape(1, 64), (128, 64)))
    s = np.arange(128)
    put("mask", (s[None, :] >= s[:, None]).astype(np.float32))
    sw = np.asarray(inp["odd_sgu_w"][0])
    put("swT", np.ascontiguousarray(sw.transpose(2, 0, 1).reshape(128, 1024)))
    sbb = np.asarray(inp["odd_sgu_b"][0]).reshape(1, 1024)
    put("sbb", np.broadcast_to(sbb, (128, 1024)))
    return c


class Sem:
    def __init__(self, h, name):
        self.h = h
        self.count = 0
        self.name = name


class Buf:
    __slots__ = ("name", "w", "r")

    def __init__(self, name):
        self.name = name
        self.w = None
        self.r = {}


class Eng:
    def __init__(self, name, sem):
        self.name = name
        self.sem = sem
        self.waited = {}
        self.prog = []


class Sched:
    def __init__(self, nc, es):
        self.nc = nc
        self.es = es
        self.engs = {}
        self.clocks = {}
        for n in ("pe", "act", "dve", "pool", "sp"):
            self.engs[n] = Eng(n, self.new_sem("sem_" + n))

    def new_sem(self, name):
        return Sem(self.es.enter_context(self.nc.semaphore(name)), name)

    def _need(self, eng, reads, writes, strict):
        need = {}

        def add(tok, same_ok):
            if tok is None:
                return
            sem, v = tok
            if sem is eng.sem and not (same_ok or strict or eng.name != "pe"):
                return
            if need.get(sem, 0) < v:
                need[sem] = v
        for b in reads:
            add(b.w, True)
        for b in writes:
            add(b.w, False)
            for sem, v in b.r.items():
                add((sem, v), False)
        return need

    def _wait(self, eng, need):
        toks = sorted(need.items(), key=lambda kv: -self.clocks[(kv[0], kv[1])][0])
        for sem, v in toks:
            if eng.waited.get(sem, 0) >= v:
                continue
            eng.prog.append(lambda h, s=sem.h, v=v: h.wait_ge(s, v))
            for s2, v2 in self.clocks[(sem, v)][1].items():
                if eng.waited.get(s2, 0) < v2:
                    eng.waited[s2] = v2

    def _deps(self, eng, reads, writes, strict=False):
        self._wait(eng, self._need(eng, reads, writes, strict))

    def _token(self, eng, sem, v):
        clk = dict(eng.waited)
        clk[sem] = v
        self.clocks[(sem, v)] = (len(self.clocks), clk)
        return (sem, v)

    def _commit(self, tok, reads, writes):
        sem, v = tok
        for b in reads:
            if b.r.get(sem, 0) < v:
                b.r[sem] = v
        for b in writes:
            b.w = tok
            b.r = {}

    def op(self, en, fn, reads=(), writes=()):
        eng = self.engs[en]
        self._deps(eng, reads, writes)
        eng.sem.count += 1
        eng.prog.append(lambda h, fn=fn, s=eng.sem.h: fn(h).then_inc(s, 1))
        tok = self._token(eng, eng.sem, eng.sem.count)
        self._commit(tok, reads, writes)
        return tok

    def mm(self, mms, reads, out_buf):
        eng = self.engs["pe"]
        self._deps(eng, reads, [out_buf])
        eng.sem.count += 1

        def run(h, mms=mms, s=eng.sem.h):
            ins = None
            for (o, l, r, st, sp) in mms:
                ins = h.matmul(o, lhsT=l, rhs=r, start=st, stop=sp)
            ins.then_inc(s, 1)
        eng.prog.append(run)
        tok = self._token(eng, eng.sem, eng.sem.count)
        self._commit(tok, reads, [out_buf])
        return tok

    def dma(self, en, out_ap, in_ap, dsem, reads=(), writes=()):
        eng = self.engs[en]
        self._deps(eng, reads, writes, strict=True)
        dsem.count += 16
        eng.prog.append(lambda h, o=out_ap, i=in_ap, s=dsem.h: h.dma_start(out=o, in_=i).then_inc(s, 16))
        tok = self._token(eng, dsem, dsem.count)
        self._commit(tok, reads, writes)
        return tok

    def wait_tokens(self, en, toks):
        eng = self.engs[en]
        need = {}
        for sem, v in toks:
            if need.get(sem, 0) < v:
                need[sem] = v
        self._wait(eng, need)


class Pool:
    def __init__(self, items):
        self.free = list(items)

    def alloc(self):
        assert self.free, "temp pool exhausted"
        return self.free.pop(0)

    def release(self, it):
        self.free.append(it)


class T:
    def __init__(self, ap, name, sem=None):
        self.ap = ap
        self.b = Buf(name)
        self.sem = sem


class Builder:
    def __init__(self, layers=(0, 1), nseq=2, tps=SEQ // TT):
        self.layers = tuple(layers)
        self.nseq = nseq
        self.tps = tps
        self.nt = nseq * tps
        self.pieces = _piece_list(self.layers)
        self.np_pass = len(self.pieces)
        self.off, self.ncst = _cst_layout()

    def act(self, out, in_, func, reads, writes, bias=0.0, scale=1.0):
        return self.S.op("act", lambda e: e.activation(out=out, in_=in_, func=func, bias=bias, scale=scale), reads, writes)

    def tt(self, en, out, in0, in1, op, reads, writes):
        return self.S.op(en, lambda e: e.tensor_tensor(out=out, in0=in0, in1=in1, op=op), reads, writes)

    def stt(self, out, in0, scalar, in1, op0, op1, reads, writes):
        return self.S.op("dve", lambda e: e.scalar_tensor_tensor(out=out, in0=in0, scalar=scalar, in1=in1, op0=op0, op1=op1), reads, writes)

    def ts(self, en, out, in0, s1, s2, op0, op1, reads, writes):
        if s2 is None:
            return self.S.op(en, lambda e: e.tensor_scalar(out=out, in0=in0, scalar1=s1, scalar2=None, op0=op0), reads, writes)
        return self.S.op(en, lambda e: e.tensor_scalar(out=out, in0=in0, scalar1=s1, scalar2=s2, op0=op0, op1=op1), reads, writes)

    def copy(self, en, out, in_, reads, writes):
        return self.S.op(en, lambda e: e.tensor_copy(out=out, in_=in_), reads, writes)

    def memset(self, en, ap, val, writes):
        return self.S.op(en, lambda e: e.memset(ap, val), (), writes)

    def rsqrt(self, y_ap, y_b, x_ap, x_b, h_ap, h_b, t_ap, t_b, iters=2, h_on_act=False):
        I32 = mybir.dt.int32
        yi, xi = y_ap.bitcast(I32), x_ap.bitcast(I32)
        self.S.op("dve", lambda e: e.tensor_scalar(out=yi, in0=xi, scalar1=-0.5, scalar2=1597463007.0, op0=ALU.mult, op1=ALU.add), [x_b], [y_b])
        if h_on_act:
            self.act(h_ap, x_ap, AF.Identity, [x_b], [h_b], scale=-0.5)
        else:
            self.ts("dve", h_ap, x_ap, -0.5, None, ALU.mult, None, [x_b], [h_b])
        for _ in range(iters):
            self.tt("dve", t_ap, y_ap, y_ap, ALU.mult, [y_b], [t_b])
            self.tt("dve", t_ap, t_ap, h_ap, ALU.mult, [t_b, h_b], [t_b])
            self.stt(y_ap, t_ap, 1.5, y_ap, ALU.add, ALU.mult, [t_b, y_b], [y_b])

    def col(self, name, j):
        o = self.off[name] + j
        return self.cst[:, o:o + 1]

    def ws_issue(self, idx):
        if idx >= self.nt * self.np_pass:
            return
        s = idx % NSLOT
        name, ncols = self.pieces[idx % self.np_pass]
        src = self.wstream[idx % self.np_pass, :, 0:ncols]
        self.S.dma("pool", self.slot[s][:, 0:ncols], src, self.slot_sem[s], reads=(), writes=[self.slot_b[s]])

    def ws_acquire(self, expect):
        idx = self.ws_next
        name, ncols = self.pieces[idx % self.np_pass]
        assert name == expect, (name, expect)
        self.ws_next += 1
        s = idx % NSLOT
        return (idx, self.slot[s], self.slot_b[s])

    def ws_release(self, pc):
        self.ws_issue(pc[0] + NSLOT)

    def bank(self):
        i = self.bank_rr
        self.bank_rr = (self.bank_rr + 1) % 6
        return self.ps[i], self.ps_b[i]

    def build(self):
        nc = bass.Bass("TRN2", target_bir_lowering=False)
        self.nc = nc
        NT = self.nt
        x_d = nc.dram_tensor("x_dev", [NT, 128, 8 * TT], F32, kind="ExternalInput").ap()
        p_d = nc.dram_tensor("p_dev", [NT, 128, 4 * TT], F32, kind="ExternalInput").ap()
        self.wstream = nc.dram_tensor("wstream", [self.np_pass, 128, SLOTC], F32, kind="ExternalInput").ap()
        cst_d = nc.dram_tensor("cst", [128, self.ncst], F32, kind="ExternalInput").ap()
        y_d = nc.dram_tensor("y_dev", [NT, 128, 8 * TT], F32, kind="ExternalOutput").ap()

        with ExitStack() as es:
            E = es.enter_context
            S = Sched(nc, es)
            self.S = S
            sb = lambda name, shape, dt: E(nc.sbuf_tensor(name, shape, dt))
            self.cst = sb("cst_sb", [128, self.ncst], F32)
            cst_b = Buf("cst")
            x32 = sb("x32", [128, 8, TT], F32)
            xbfs = [sb(f"xbf{i}", [128, 8, TT], BF16) for i in range(2)]
            pbfs = [sb(f"pbf{i}", [128, 4, TT], BF16) for i in range(2)]
            arena = sb("arena", [128, 32 * 512], BF16)
            hbuf = sb("hbuf", [128, 4, 16 + TT], F32)
            cvbuf = sb("cvbuf", [128, 4, 2 + TT], F32)
            fhalo = sb("fhalo", [128, 2, NFF, 2], F32)
            wmT = sb("wmT", [128, 8, 128], BF16)
            Cc = sb("Cc", [128, 16, 128], F32)
            onesM = sb("onesM", [128, 128], BF16)
            ones = sb("ones", [128, 128], BF16)
            hbias = sb("hbias", [128, 16], F32)
            t32 = sb("t32", [128, NT32, TT], F32)
            za = sb("za", [128, 3, 2 + TT], F32)
            s528 = sb("s528", [128, 2, 16 + TT], F32)
            rsq = sb("rsq", [128, 3, TT], BF16)
            vraw = sb("vraw", [128, 2, 2048], F32)
            stt_ = sb("bnst", [128, 2, 4, 6], F32)
            mvt = sb("mvt", [128, 2, 8], F32)
            self.slot = [sb(f"slot{i}", [128, SLOTC], BF16) for i in range(NSLOT)]
            self.slot_b = [Buf(f"slot{i}") for i in range(NSLOT)]
            self.slot_sem = [S.new_sem(f"sem_slot{i}") for i in range(NSLOT)]
            self.ps = [E(nc.psum_tensor(f"ps{i}", [128, 512], F32)) for i in range(8)]
            self.ps_b = [Buf(f"ps{i}") for i in range(8)]
            self.bank_rr = 0
            sem_x32 = S.new_sem("sem_x32")
            sem_xbf = [S.new_sem(f"sem_xbf{i}") for i in range(2)]
            sem_pbf = [S.new_sem(f"sem_pbf{i}") for i in range(2)]
            sem_cst = S.new_sem("sem_cst")

            x32_b = [Buf(f"x32_{m}") for m in range(8)]
            xbf_bs = [[Buf(f"xbf{i}_{m}") for m in range(8)] for i in range(2)]
            pbf_bs = [Buf("pbf0"), Buf("pbf1")]
            xbf, xbf_b, pbf, pbf_b = xbfs[0], xbf_bs[0], pbfs[0], pbf_bs[0]
            ar_b = [Buf(f"ar{i}") for i in range(32)]
            hb_b = [Buf(f"hb{g}") for g in range(4)]
            cv_b = [Buf(f"cv{j}") for j in range(4)]
            fh_b = [[Buf(f"fh{l}_{c}") for c in range(NFF)] for l in range(2)]
            wm_b = Buf("wmT")
            cc_b = Buf("Cc")
            k_b = Buf("consts")
            tp = Pool([T(t32[:, i, :], f"t32_{i}", S.new_sem(f"sem_t{i}")) for i in range(NT32)])
            zap = Pool([(za[:, i, :], Buf(f"zah{i}"), Buf(f"zam{i}")) for i in range(3)])
            rsqp = Pool([T(rsq[:, i, :], f"rsq{i}") for i in range(3)])
            s5_b = [Buf("s528a"), Buf("s528b")]
            vr_b = [[Buf(f"vr{i}_{q}") for q in range(4)] for i in range(2)]
            st_b = [Buf("bnst0"), Buf("bnst1")]
            mv_b = [Buf("mv0"), Buf("mv1")]
            arv = lambda i: arena[:, i * 512:(i + 1) * 512]
            out_toks = []

            S.dma("sp", self.cst[:], cst_d, sem_cst, (), [cst_b])
            self.ws_next = 0
            for i in range(NSLOT):
                self.ws_issue(i)
            self.memset("dve", onesM[:], 1.0 / 1024.0, [k_b])
            self.memset("dve", ones[:], 1.0, [k_b])
            o = self.off
            for l in (0, 1):
                self.ts("dve", hbias[:, l * 8:(l + 1) * 8], self.cst[:, o[f"pb{l}"]:o[f"pb{l}"] + 8], 0.5, None, ALU.mult, None, [cst_b], [k_b])
            if 1 in self.layers:
                swv = self.cst[:, o["swT"]:o["swT"] + 1024].rearrange("p (h t) -> p h t", h=8)
                mk = self.cst[:, o["mask"]:o["mask"] + 128].unsqueeze(1).to_broadcast([128, 8, 128])
                self.tt("dve", wmT[:], swv, mk, ALU.mult, [cst_b], [wm_b])
                for hh in range(2):
                    bk, bkb = self.bank()
                    mms = [(bk[:, j * 128:(j + 1) * 128], ones[:], wmT[:, hh * 4 + j, :], True, True) for j in range(4)]
                    S.mm(mms, [k_b, wm_b], bkb)
                    for j in range(4):
                        h = hh * 4 + j
                        for e in range(2):
                            dc = 2 * h + e
                            self.stt(Cc[:, dc, :], bk[:, j * 128:(j + 1) * 128], self.col("sb", dc),
                                     self.cst[:, o["sbb"] + h * 128:o["sbb"] + (h + 1) * 128], ALU.mult, ALU.add,
                                     [bkb, cst_b], [cc_b])

            ln_pending = []

            def ln_flush():
                while ln_pending:
                    ln_pending.pop(0)()

            def ln_evac(m, bk, bkb):
                self.stt(x32[:, m, :], x32[:, m, :], ALPHA, bk[:, 0:TT], ALU.mult, ALU.add, [x32_b[m], bkb], [x32_b[m]])
                self.act(xbf[:, m, :], x32[:, m, :], AF.Identity, [x32_b[m]], [xbf_b[m]])
                rq = rsqp.alloc()
                self.act(rq.ap, x32[:, m, :], AF.Square, [x32_b[m]], [rq.b])
                ln_flush()

                def pend(m=m, rq=rq):
                    S.mm([(self.ps[6][:, 0:TT], onesM[:], xbf[:, m, :], m == 0, m == 7)], [k_b, xbf_b[m]], self.ps_b[6])
                    S.mm([(self.ps[7][:, 0:TT], onesM[:], rq.ap, m == 0, m == 7)], [k_b, rq.b], self.ps_b[7])
                    rsqp.release(rq)
                ln_pending.append(pend)

            def ln_norm(gname, bname):
                ln_flush()
                mean_sb, msq, ve, Aa, Bb = [tp.alloc() for _ in range(5)]
                b6, b7 = self.ps_b[6], self.ps_b[7]
                self.act(msq.ap, self.ps[6][:, 0:TT], AF.Square, [b6], [msq.b])
                self.act(mean_sb.ap, self.ps[6][:, 0:TT], AF.Identity, [b6], [mean_sb.b])
                self.stt(ve.ap, self.ps[7][:, 0:TT], EPS, msq.ap, ALU.add, ALU.subtract, [b7, msq.b], [ve.b])
                hh, tq = tp.alloc(), tp.alloc()
                c0 = tp.alloc()
                self.tt("dve", c0.ap, x32[:, 0, :], self.ps[6][:, 0:TT], ALU.subtract, [x32_b[0], b6], [c0.b])
                wflat = wmT[:].rearrange("p h t -> p (h t)")[:, 0:512]

                def warm(n, gate):
                    bkw, bkwb = self.ps[self.bank_rr], self.ps_b[self.bank_rr]
                    S.mm([(bkw[:, 0:512], onesM[:], wflat, True, True) for _ in range(n)], [k_b, wm_b] + gate, bkwb)
                warm(6, [ve.b])
                self.rsqrt(Aa.ap, Aa.b, ve.ap, ve.b, hh.ap, hh.b, tq.ap, tq.b, iters=1, h_on_act=True)
                warm(6, [Aa.b])
                self.tt("dve", tq.ap, Aa.ap, Aa.ap, ALU.mult, [Aa.b], [tq.b])
                self.tt("dve", tq.ap, tq.ap, hh.ap, ALU.mult, [tq.b, hh.b], [tq.b])
                self.stt(Aa.ap, tq.ap, 1.5, Aa.ap, ALU.add, ALU.mult, [tq.b, Aa.b], [Aa.b])
                warm(4, [Aa.b])
                tp.release(hh)
                tp.release(tq)
                self.tt("dve", c0.ap, c0.ap, Aa.ap, ALU.mult, [c0.b, Aa.b], [c0.b])
                self.act(xbf[:, 0, :], c0.ap, AF.Identity, [c0.b, cst_b], [xbf_b[0]], bias=self.col(bname, 0), scale=self.col(gname, 0))
                self.act(x32[:, 0, :], c0.ap, AF.Identity, [c0.b, cst_b], [x32_b[0]], bias=self.col(bname, 0), scale=self.col(gname, 0))
                tp.release(c0)
                self.stt(Bb.ap, mean_sb.ap, -1.0, Aa.ap, ALU.mult, ALU.mult, [mean_sb.b, Aa.b], [Bb.b])
                for m0, nk in ((1, 2), (3, 2), (5, 2), (7, 1)):
                    ts_ = [tp.alloc() for _ in range(nk)]
                    for k in range(nk):
                        self.tt("dve", ts_[k].ap, x32[:, m0 + k, :], Aa.ap, ALU.mult, [x32_b[m0 + k], Aa.b], [ts_[k].b])
                    for k in range(nk):
                        self.tt("dve", ts_[k].ap, ts_[k].ap, Bb.ap, ALU.add, [ts_[k].b, Bb.b], [ts_[k].b])
                    for k in range(nk):
                        m = m0 + k
                        self.act(xbf[:, m, :], ts_[k].ap, AF.Identity, [ts_[k].b, cst_b], [xbf_b[m]], bias=self.col(bname, m), scale=self.col(gname, m))
                    for k in range(nk):
                        m = m0 + k
                        self.act(x32[:, m, :], ts_[k].ap, AF.Identity, [ts_[k].b, cst_b], [x32_b[m]], bias=self.col(bname, m), scale=self.col(gname, m))
                        tp.release(ts_[k])
                for t in (mean_sb, msq, ve, Aa, Bb):
                    tp.release(t)

            def ffn(l, li):
                fcw = lambda k, c: self.col(f"fcw{l}", k * NFF + c)
                up_tail = [None]
                for j in range(11):
                    pc = self.ws_acquire("up")
                    _, sl, slb = pc
                    if j == 0:
                        pre = [self.bank() for _ in range(4)]
                        for kc in range(8):
                            for i4 in range(4):
                                S.mm([(pre[i4][0][:, 0:TT], sl[:, kc * 512 + i4 * 128: kc * 512 + (i4 + 1) * 128], xbf[:, kc, :], kc == 0, kc == 7)],
                                     [slb, xbf_b[kc]], pre[i4][1])
                    for jj in range(2):
                        c = 2 * j + jj
                        if j == 0:
                            (bA, bAb), (bG, bGb) = pre[jj], pre[2 + jj]
                        else:
                            bA, bAb = self.bank()
                            bG, bGb = self.bank()
                            S.mm([(bA[:, 0:TT], sl[:, kc * 512 + jj * 128: kc * 512 + (jj + 1) * 128], xbf[:, kc, :], kc == 0, kc == 7) for kc in range(8)],
                                 [slb] + xbf_b, bAb)
                            S.mm([(bG[:, 0:TT], sl[:, kc * 512 + 256 + jj * 128: kc * 512 + 256 + (jj + 1) * 128], xbf[:, kc, :], kc == 0, kc == 7) for kc in range(8)],
                                 [slb] + xbf_b, bGb)
                        zt, zhb, zmb = zap.alloc()
                        fb = fh_b[li][c]
                        self.copy("pool", zt[:, 0:2], fhalo[:, li, c, :], [fb], [zhb])
                        self.act(zt[:, 2:2 + TT], bA[:, 0:TT], AF.Identity, [bAb], [zmb])
                        self.copy("pool", fhalo[:, li, c, :], zt[:, TT:TT + 2], [zmb], [fb])
                        t0 = tp.alloc()
                        self.act(t0.ap, zt[:, 0:TT], AF.Identity, [zhb, zmb, cst_b], [t0.b], bias=self.col(f"fcb{l}", c), scale=fcw(0, c))
                        self.stt(t0.ap, zt[:, 1:1 + TT], fcw(1, c), t0.ap, ALU.mult, ALU.add, [zhb, zmb, t0.b, cst_b], [t0.b])
                        self.stt(t0.ap, zt[:, 2:2 + TT], fcw(2, c), t0.ap, ALU.mult, ALU.add, [zmb, t0.b, cst_b], [t0.b])
                        zap.release((zt, zhb, zmb))
                        prev_tail = up_tail[0]

                        def tail(t0=t0, bG=bG, bGb=bGb, c=c):
                            self.act(t0.ap, t0.ap, AF.Gelu, [t0.b], [t0.b])
                            self.tt("dve", arv(c), t0.ap, bG[:, 0:TT], ALU.mult, [t0.b, bGb], [ar_b[c]])
                            tp.release(t0)
                        up_tail[0] = tail
                        if prev_tail is not None:
                            prev_tail()
                    self.ws_release(pc)
                up_tail[0]()
                up_tail[0] = None
                for m in range(8):
                    pc = self.ws_acquire("down")
                    _, sl, slb = pc
                    bk, bkb = self.bank()
                    if m == 0:
                        for kc in range(NFF):
                            S.mm([(bk[:, 0:TT], sl[:, kc * 128:(kc + 1) * 128], arv(kc), kc == 0, kc == NFF - 1)], [slb, ar_b[kc]], bkb)
                    else:
                        S.mm([(bk[:, 0:TT], sl[:, kc * 128:(kc + 1) * 128], arv(kc), kc == 0, kc == NFF - 1) for kc in range(NFF)],
                             [slb] + ar_b[0:NFF], bkb)
                    self.ws_release(pc)
                    ln_evac(m, bk, bkb)
                ln_norm(f"fg{l}", f"fb{l}")

            def ple(l, li, tile, last):
                g0 = self.ws_acquire("gate")
                pj = self.ws_acquire("proj")
                g1 = self.ws_acquire("gate")
                pre = [self.bank() for _ in range(4)]
                for kc in range(8):
                    for i4 in range(4):
                        S.mm([(pre[i4][0][:, 0:TT], g0[1][:, kc * 512 + i4 * 128: kc * 512 + (i4 + 1) * 128], xbf[:, kc, :], kc == 0, kc == 7)],
                             [g0[2], xbf_b[kc]], pre[i4][1])
                for m in range(8):
                    gp = g0 if m < 4 else g1
                    mmi = m % 4
                    if m < 4:
                        bT, bTb = pre[m]
                    else:
                        bT, bTb = self.bank()
                    bP, bPb = self.bank()
                    if m >= 4:
                        S.mm([(bT[:, 0:TT], gp[1][:, kc * 512 + mmi * 128: kc * 512 + (mmi + 1) * 128], xbf[:, kc, :], kc == 0, kc == 7) for kc in range(8)],
                             [gp[2]] + xbf_b, bTb)
                    S.mm([(bP[:, 0:TT], pj[1][:, kc * 1024 + m * 128: kc * 1024 + (m + 1) * 128], pbf[:, li * 2 + kc, :], kc == 0, kc == 1) for kc in range(2)],
                         [pj[2], pbf_b], bPb)
                    if m == 3:
                        self.ws_release(g0)
                    tg = tp.alloc()
                    g2 = tp.alloc()
                    self.act(tg.ap, bT[:, 0:TT], AF.Tanh, [bTb, k_b], [tg.b], bias=hbias[:, l * 8 + m: l * 8 + m + 1], scale=0.5)
                    self.stt(g2.ap, tg.ap, 1.0, bP[:, 0:TT], ALU.add, ALU.mult, [tg.b, bPb], [g2.b])
                    if last:
                        xo = tp.alloc()
                        self.stt(xo.ap, g2.ap, 0.5, x32[:, m, :], ALU.mult, ALU.add, [g2.b, x32_b[m]], [xo.b])
                        out_toks.append(S.dma("sp", y_d[tile, :, m * TT:(m + 1) * TT], xo.ap, xo.sem, reads=[xo.b], writes=()))
                        tp.release(xo)
                    else:
                        self.stt(x32[:, m, :], g2.ap, 0.5, x32[:, m, :], ALU.mult, ALU.add, [g2.b, x32_b[m]], [x32_b[m]])
                    tp.release(tg)
                    tp.release(g2)
                if not last:
                    for m in range(8):
                        self.act(xbf[:, m, :], x32[:, m, :], AF.Identity, [x32_b[m]], [xbf_b[m]])
                self.ws_release(pj)
                self.ws_release(g1)

            def mixer0(first):
                cw = lambda k, j: self.col("cw", k * 4 + j)
                pc = self.ws_acquire("in0")
                W = 16 + TT
                for g in range(4):
                    w = POOL_W[g]
                    bk, bkb = self.bank()
                    S.mm([(bk[:, 0:TT], pc[1][:, kc * 512 + g * 128: kc * 512 + (g + 1) * 128], xbf[:, kc, :], kc == 0, kc == 7) for kc in range(8)],
                         [pc[2]] + xbf_b, bkb)
                    self.act(hbuf[:, g, 16:W], bk[:, 0:TT], AF.Identity, [bkb], [hb_b[g]])
                    prev, prevb, lo, k, idx = hbuf[:, g, :], hb_b[g], 0, 1, 0
                    while k < w:
                        new, newb = s528[:, idx, :], s5_b[idx]
                        self.tt("dve", new[:, lo + k:W], prev[:, lo + k:W], prev[:, lo:W - k], ALU.add, [prevb], [newb])
                        prev, prevb, lo, k, idx = new, newb, lo + k, k * 2, 1 - idx
                    self.stt(arv(8 + g), prev[:, 16:W], 1.0 / w, hbuf[:, g, 16:W], ALU.mult, ALU.subtract, [prevb, hb_b[g]], [ar_b[8 + g]])
                    if first:
                        t = tp.alloc()
                        ic = self.cst[:, self.off["invcnt"] + g * 16: self.off["invcnt"] + (g + 1) * 16]
                        self.tt("dve", t.ap[:, 0:16], prev[:, 16:32], ic, ALU.mult, [prevb, cst_b], [t.b])
                        self.tt("dve", arv(8 + g)[:, 0:16], t.ap[:, 0:16], hbuf[:, g, 16:32], ALU.subtract, [t.b, hb_b[g]], [ar_b[8 + g]])
                        tp.release(t)
                    self.copy("pool", hbuf[:, g, 0:16], hbuf[:, g, TT:TT + 16], [hb_b[g]], [hb_b[g]])
                self.ws_release(pc)
                pc = self.ws_acquire("in0")
                cg = []
                for j in range(4):
                    bk, bkb = self.bank()
                    S.mm([(bk[:, 0:TT], pc[1][:, kc * 512 + j * 128: kc * 512 + (j + 1) * 128], xbf[:, kc, :], kc == 0, kc == 7) for kc in range(8)],
                         [pc[2]] + xbf_b, bkb)
                    t = tp.alloc()
                    self.act(t.ap, bk[:, 0:TT], AF.Identity, [bkb], [t.b])
                    cg.append(t)
                self.ws_release(pc)
                pc = self.ws_acquire("in0")
                cvt = []
                for j in range(4):
                    bk, bkb = self.bank()
                    S.mm([(bk[:, 0:TT], pc[1][:, kc * 512 + j * 128: kc * 512 + (j + 1) * 128], xbf[:, kc, :], kc == 0, kc == 7) for kc in range(8)],
                         [pc[2]] + xbf_b, bkb)
                    self.tt("dve", cvbuf[:, j, 2:2 + TT], cg[j].ap, bk[:, 0:TT], ALU.mult, [cg[j].b, bkb], [cv_b[j]])
                    tp.release(cg[j])
                    t = tp.alloc()
                    self.ts("dve", t.ap, cvbuf[:, j, 0:TT], cw(0, j), None, ALU.mult, None, [cv_b[j], cst_b], [t.b])
                    self.stt(t.ap, cvbuf[:, j, 1:1 + TT], cw(1, j), t.ap, ALU.mult, ALU.add, [cv_b[j], t.b, cst_b], [t.b])
                    self.stt(t.ap, cvbuf[:, j, 2:2 + TT], cw(2, j), t.ap, ALU.mult, ALU.add, [cv_b[j], t.b, cst_b], [t.b])
                    self.copy("pool", cvbuf[:, j, 0:2], cvbuf[:, j, TT:TT + 2], [cv_b[j]], [cv_b[j]])
                    cvt.append(t)
                self.ws_release(pc)
                pc = self.ws_acquire("in0")
                for j in range(4):
                    bk, bkb = self.bank()
                    S.mm([(bk[:, 0:TT], pc[1][:, kc * 512 + j * 128: kc * 512 + (j + 1) * 128], xbf[:, kc, :], kc == 0, kc == 7) for kc in range(8)],
                         [pc[2]] + xbf_b, bkb)
                    self.tt("dve", arv(4 + j), cvt[j].ap, bk[:, 0:TT], ALU.mult, [cvt[j].b, bkb], [ar_b[4 + j]])
                    tp.release(cvt[j])
                self.ws_release(pc)
                pc = self.ws_acquire("pmix")
                for g in range(4):
                    bk, bkb = self.bank()
                    S.mm([(bk[:, 0:TT], pc[1][:, g * 128:(g + 1) * 128], arv(8 + g), True, True)], [pc[2], ar_b[8 + g]], bkb)
                    self.act(arv(g), bk[:, 0:TT], AF.Identity, [bkb, cst_b], [ar_b[g]], scale=self.col("psc", g))
                self.ws_release(pc)
                for q in range(2):
                    pc = self.ws_acquire("out0")
                    for mmi in range(4):
                        m = q * 4 + mmi
                        bk, bkb = self.bank()
                        S.mm([(bk[:, 0:TT], pc[1][:, kc * 512 + mmi * 128: kc * 512 + (mmi + 1) * 128], arv(kc), kc == 0, kc == 7) for kc in range(8)],
                             [pc[2]] + ar_b[0:8], bkb)
                        ln_evac(m, bk, bkb)
                    self.ws_release(pc)
                ln_norm("mg0", "mb0")

            def mixer1():
                NCH = TT // 128
                vp = [self.ws_acquire("v") for _ in range(4)]
                pre = [self.bank() for _ in range(4)]
                for kc in range(8):
                    for q in range(4):
                        S.mm([(pre[q][0][:, 0:512], xbf[:, kc, 0:128], vp[q][1][:, kc * 512:(kc + 1) * 512], kc == 0, kc == 7)],
                             [vp[q][2], xbf_b[kc]], pre[q][1])
                ug_all = {}

                def u_groups(q4):
                    up = self.ws_acquire("u")
                    for mmi in range(4):
                        bU, bUb = self.bank()
                        S.mm([(bU[:, 0:TT], up[1][:, kc * 512 + mmi * 128: kc * 512 + (mmi + 1) * 128], xbf[:, kc, :], kc == 0, kc == 7) for kc in range(8)],
                             [up[2]] + xbf_b, bUb)
                        ug = tp.alloc()
                        self.act(ug.ap, bU[:, 0:TT], AF.Gelu, [bUb], [ug.b])
                        ug_all[q4 * 4 + mmi] = ug
                    self.ws_release(up)

                def s_groups(q4):
                    for mmi in range(4):
                        dc = q4 * 4 + mmi
                        h = dc // 2
                        bS, bSb = self.bank()
                        S.mm([(bS[:, c * 128:(c + 1) * 128], arena[:, c * 2048 + dc * 128: c * 2048 + (dc + 1) * 128], wmT[:, h, :], True, True) for c in range(NCH)],
                             [wm_b] + ar_b[0:4 * NCH], bSb)
                        mt = tp.alloc()
                        self.stt(mt.ap.rearrange("p (c t) -> p c t", c=NCH), bS[:, 0:TT].rearrange("p (c t) -> p c t", c=NCH), self.col("sg", dc),
                                 Cc[:, dc, :].unsqueeze(1).to_broadcast([128, NCH, 128]), ALU.mult, ALU.add, [bSb, cc_b, cst_b], [mt.b])
                        ug = ug_all.pop(dc)
                        self.tt("dve", arv(16 + dc), mt.ap, ug.ap, ALU.mult, [mt.b, ug.b], [ar_b[16 + dc]])
                        tp.release(ug)
                        tp.release(mt)

                for c in range(NCH):
                    vi = c % 2
                    if c == NCH - 1:
                        u_groups(0)
                    for q in range(4):
                        if c == 0:
                            bk, bkb = pre[q]
                        else:
                            bk, bkb = self.bank()
                            S.mm([(bk[:, 0:512], xbf[:, kc, c * 128:(c + 1) * 128], vp[q][1][:, kc * 512:(kc + 1) * 512], kc == 0, kc == 7) for kc in range(8)],
                                 [vp[q][2]] + xbf_b, bkb)
                        self.act(vraw[:, vi, q * 512:(q + 1) * 512], bk[:, 0:512], AF.Gelu, [bkb], [vr_b[vi][q]])
                        S.op("dve", lambda e, vi=vi, q=q: e.bn_stats(out=stt_[:, vi, q, :], in_=vraw[:, vi, q * 512:(q + 1) * 512]), [vr_b[vi][q]], [st_b[vi]])
                        if c == NCH - 1:
                            self.ws_release(vp[q])
                    S.op("dve", lambda e, vi=vi: e.bn_aggr(out=mvt[:, vi, 0:2], in_=stt_[:, vi, :, :]), [st_b[vi]], [mv_b[vi]])
                    self.ts("dve", mvt[:, vi, 2:3], mvt[:, vi, 1:2], EPS, None, ALU.add, None, [mv_b[vi]], [mv_b[vi]])
                    self.rsqrt(mvt[:, vi, 3:4], mv_b[vi], mvt[:, vi, 2:3], mv_b[vi], mvt[:, vi, 4:5], mv_b[vi], mvt[:, vi, 5:6], mv_b[vi])
                    self.ts("dve", arena[:, c * 2048:(c + 1) * 2048], vraw[:, vi, :], mvt[:, vi, 0:1], mvt[:, vi, 3:4], ALU.subtract, ALU.mult,
                            vr_b[vi] + [mv_b[vi]], ar_b[4 * c:4 * c + 4])
                u_groups(1)
                s_groups(0)
                s_groups(1)
                for q4 in (2, 3):
                    u_groups(q4)
                    s_groups(q4)
                for q in range(4):
                    pc = self.ws_acquire("out1")
                    for mmi in range(2):
                        m = q * 2 + mmi
                        bk, bkb = self.bank()
                        S.mm([(bk[:, 0:TT], pc[1][:, kc * 256 + mmi * 128: kc * 256 + (mmi + 1) * 128], arv(16 + kc), kc == 0, kc == 15) for kc in range(16)],
                             [pc[2]] + ar_b[16:32], bkb)
                        ln_evac(m, bk, bkb)
                    self.ws_release(pc)
                ln_norm("mg1", "mb1")

            def load_bf(t):
                i = t % 2
                S.dma("pool", xbfs[i][:].rearrange("p c t -> p (c t)"), x_d[t], sem_xbf[i], (), xbf_bs[i])
                S.dma("pool", pbfs[i][:].rearrange("p c t -> p (c t)"), p_d[t], sem_pbf[i], (), [pbf_bs[i]])

            load_bf(0)
            for tile in range(NT):
                first = (tile % self.tps) == 0
                xbf, xbf_b, pbf, pbf_b = xbfs[tile % 2], xbf_bs[tile % 2], pbfs[tile % 2], pbf_bs[tile % 2]
                S.dma("sp", x32[:].rearrange("p c t -> p (c t)"), x_d[tile], sem_x32, (), x32_b)
                if tile + 1 < NT:
                    load_bf(tile + 1)
                if first:
                    self.memset("pool", hbuf[:, :, 0:16], 0.0, hb_b)
                    self.memset("pool", cvbuf[:, :, 0:2], 0.0, cv_b)
                    self.memset("pool", fhalo[:].rearrange("p a b c -> p (a b c)"), 0.0, fh_b[0] + fh_b[1])
                for li, l in enumerate(self.layers):
                    if l == 0:
                        mixer0(first)
                    else:
                        mixer1()
                    ffn(l, li)
                    ple(l, li, tile, li == len(self.layers) - 1)
            assert self.ws_next == NT * self.np_pass
            self.sbuf_left = nc.sbuf_bytes_remaining
            S.wait_tokens("sp", out_toks)

            with nc.Block() as block:
                @block.tensor
                def _(h):
                    for c in S.engs["pe"].prog:
                        c(h)

                @block.scalar
                def _(h):
                    for c in S.engs["act"].prog:
                        c(h)

                @block.vector
                def _(h):
                    for c in S.engs["dve"].prog:
                        c(h)

                @block.gpsimd
                def _(h):
                    for c in S.engs["pool"].prog:
                        c(h)

                @block.sync
                def _(h):
                    for c in S.engs["sp"].prog:
                        c(h)
        return nc


def _x_to_dev(xs):
    n, S_, _ = xs.shape
    a = xs.reshape(n, S_ // TT, TT, 8, 128).transpose(0, 1, 4, 3, 2)
    return np.ascontiguousarray(a).reshape(n * (S_ // TT), 128, 8 * TT)


def _p_to_dev(ps):
    L, n, S_, _ = ps.shape
    a = ps.reshape(L, n, S_ // TT, TT, 2, 128).transpose(1, 2, 5, 0, 4, 3)
    return np.ascontiguousarray(a).reshape(n * (S_ // TT), 128, L * 2 * TT)


def _y_from_dev(y, n, S_):
    a = y.reshape(n, S_ // TT, 128, 8, TT).transpose(0, 1, 4, 3, 2)
    return np.ascontiguousarray(a).reshape(n, S_, 1024)


def run(inputs, layers=(0, 1), ncores=NCORES, nseq=2, seqlen=SEQ):
    inp = {k: np.asarray(v) for k, v in inputs.items()}
    x = inp["x"].astype(np.float32, copy=False)
    p = inp["p"].astype(np.float32, copy=False)
    b = Builder(layers=layers, nseq=nseq, tps=seqlen // TT)
    nc = b.build()
    ws = _build_wstream(inp, layers)
    cst = _build_cst(inp)
    psel = p[list(layers)]
    if len(layers) == 1:
        psel = np.concatenate([psel, np.zeros_like(psel)], 0)
    in_maps = []
    for c in range(ncores):
        sl = slice(c * nseq, (c + 1) * nseq)
        in_maps.append({
            "x_dev": _x_to_dev(x[sl, :seqlen]),
            "p_dev": _p_to_dev(psel[:, sl, :seqlen]),
            "wstream": ws,
            "cst": cst,
        })
    res = run_bass_kernel_spmd(nc, in_maps, core_ids=list(range(ncores)))
    outs = [_y_from_dev(np.asarray(r["y_dev"]), nseq, seqlen) for r in res.results]
    return np.concatenate(outs, 0).astype(np.float32)


def kernel(**inputs):
    return run(inputs)
```

```python
import numpy as np
from contextlib import ExitStack
import concourse.bass as bass
import concourse.mybir as mybir
from concourse.bass_utils import run_bass_kernel_spmd

F32 = mybir.dt.float32
BF16 = mybir.dt.bfloat16
ALU = mybir.AluOpType
AF = mybir.ActivationFunctionType

D = 1024
SEQ = 4096
BATCH = 16
NCORES = 8
TT = 512
DFF = 2816
NFF = DFF // 128
PLE = 256
ALPHA = float((2 * 2) ** 0.25)
EPS = 1e-5
POOL_W = (2, 4, 8, 16)
NSLOT = 6
SLOTC = 4096
NT32 = 10


def _kxn(W, cols):
    K = W.shape[0]
    sub = np.asarray(W)[:, cols]
    return np.ascontiguousarray(sub.reshape(K // 128, 128, -1).transpose(1, 0, 2).reshape(128, -1))


def _piece_list(layers):
    out = []
    for l in layers:
        if l == 0:
            out += [("in0", 4096)] * 4 + [("pmix", 512)] + [("out0", 4096)] * 2
        else:
            out += [("v", 4096)] * 4 + [("u", 4096)] * 4 + [("out1", 4096)] * 4
        out += [("up", 4096)] * 11 + [("down", NFF * 128)] * 8
        out += [("gate", 4096), ("proj", 2048), ("gate", 4096)]
    return out


def _build_wstream(inp, layers):
    ar = np.arange
    pcs = []
    for l in layers:
        if l == 0:
            W = inp["even_w_in"][0]
            for c0 in (0, 1024, 1536, 512):
                pcs.append(_kxn(W, ar(c0, c0 + 512)))
            pm = np.asarray(inp["even_pool_mix"][0])
            pcs.append(np.ascontiguousarray(pm.transpose(1, 0, 2).reshape(128, 512)))
            W = inp["even_w_out"][0]
            for q in range(2):
                pcs.append(_kxn(W, ar(q * 512, (q + 1) * 512)))
        else:
            W = inp["odd_w_in"][0]
            for q in range(4):
                pcs.append(_kxn(W, ar(2048 + q * 512, 2048 + (q + 1) * 512)))
            for q in range(4):
                pcs.append(_kxn(W, ar(q * 512, (q + 1) * 512)))
            W = inp["odd_w_out"][0]
            for q in range(4):
                pcs.append(_kxn(W, ar(q * 256, (q + 1) * 256)))
        W = inp["ffn_w_up"][l]
        for j in range(11):
            cols = np.concatenate([ar(2 * j * 128, (2 * j + 2) * 128), DFF + ar(2 * j * 128, (2 * j + 2) * 128)])
            pcs.append(_kxn(W, cols))
        W = inp["ffn_w_down"][l]
        for m in range(8):
            pcs.append(_kxn(W, ar(m * 128, (m + 1) * 128)))
        Wg = inp["ple_w_gate"][l]
        Wp = inp["ple_w_proj"][l]
        pcs.append(_kxn(Wg, ar(0, 512)))
        pcs.append(_kxn(Wp, ar(0, 1024)))
        pcs.append(_kxn(Wg, ar(512, 1024)))
    ws = np.zeros((len(pcs), 128, SLOTC), np.float32)
    for i, a in enumerate(pcs):
        ws[i, :, : a.shape[1]] = a
    return ws


def _cst_layout():
    off = {}
    n = 0

    def add(name, w):
        nonlocal n
        off[name] = n
        n += w
    for l in (0, 1):
        for nm, w in (("mg", 8), ("mb", 8), ("fg", 8), ("fb", 8), ("fcw", 66), ("fcb", 22), ("pb", 8)):
            add(f"{nm}{l}", w)
    add("psc", 4)
    add("cw", 12)
    add("sg", 16)
    add("sb", 16)
    add("invcnt", 64)
    add("mask", 128)
    add("swT", 1024)
    add("sbb", 1024)
    return off, n


def _vcols(v):
    v = np.asarray(v, np.float32)
    return v.reshape(-1, 128).T


def _build_cst(inp):
    off, n = _cst_layout()
    c = np.zeros((128, n), np.float32)

    def put(name, arr):
        c[:, off[name]: off[name] + arr.shape[1]] = arr
    for l in (0, 1):
        put(f"mg{l}", _vcols(inp["mix_ln_g"][l]))
        put(f"mb{l}", _vcols(inp["mix_ln_b"][l]))
        put(f"fg{l}", _vcols(inp["ffn_ln_g"][l]))
        put(f"fb{l}", _vcols(inp["ffn_ln_b"][l]))
        cw = np.asarray(inp["ffn_conv_w"][l])
        put(f"fcw{l}", np.concatenate([_vcols(cw[k]) for k in range(3)], 1))
        put(f"fcb{l}", _vcols(inp["ffn_conv_b"][l]))
        put(f"pb{l}", _vcols(inp["ple_b_gate"][l]))
    put("psc", _vcols(inp["even_pool_scale"][0]))
    cw = np.asarray(inp["even_conv_w"][0])
    put("cw", np.concatenate([_vcols(cw[k]) for k in range(3)], 1))
    put("sg", _vcols(inp["odd_sgu_ln_g"][0]))
    put("sb", _vcols(inp["odd_sgu_ln_b"][0]))
    t = np.arange(16, dtype=np.float32)
    ic = np.stack([1.0 / np.minimum(t + 1.0, float(w)) for w in POOL_W], 0).astype(np.float32)
    put("invcnt", np.broadcast_to(ic.reshape(1, 64), (128, 64)))
    s = np.arange(128)
    put("mask", (s[None, :] >= s[:, None]).astype(np.float32))
    sw = np.asarray(inp["odd_sgu_w"][0])
    put("swT", np.ascontiguousarray(sw.transpose(2, 0, 1).reshape(128, 1024)))
    sbb = np.asarray(inp["odd_sgu_b"][0]).reshape(1, 1024)
    put("sbb", np.broadcast_to(sbb, (128, 1024)))
    return c


class Sem:
    def __init__(self, h, name):
        self.h = h
        self.count = 0
        self.name = name


class Buf:
    __slots__ = ("name", "w", "r")

    def __init__(self, name):
        self.name = name
        self.w = None
        self.r = {}


class Eng:
    def __init__(self, name, sem):
        self.name = name
        self.sem = sem
        self.waited = {}
        self.prog = []


class Sched:
    def __init__(self, nc, es):
        self.nc = nc
        self.es = es
        self.engs = {}
        self.clocks = {}
        for n in ("pe", "act", "dve", "pool", "sp"):
            self.engs[n] = Eng(n, self.new_sem("sem_" + n))

    def new_sem(self, name):
        return Sem(self.es.enter_context(self.nc.semaphore(name)), name)

    def _need(self, eng, reads, writes, strict):
        need = {}

        def add(tok, same_ok):
            if tok is None:
                return
            sem, v = tok
            if sem is eng.sem and not (same_ok or strict or eng.name != "pe"):
                return
            if need.get(sem, 0) < v:
                need[sem] = v
        for b in reads:
            add(b.w, True)
        for b in writes:
            add(b.w, False)
            for sem, v in b.r.items():
                add((sem, v), False)
        return need

    def _wait(self, eng, need):
        toks = sorted(need.items(), key=lambda kv: -self.clocks[(kv[0], kv[1])][0])
        for sem, v in toks:
            if eng.waited.get(sem, 0) >= v:
                continue
            eng.prog.append(lambda h, s=sem.h, v=v: h.wait_ge(s, v))
            for s2, v2 in self.clocks[(sem, v)][1].items():
                if eng.waited.get(s2, 0) < v2:
                    eng.waited[s2] = v2

    def _deps(self, eng, reads, writes, strict=False):
        self._wait(eng, self._need(eng, reads, writes, strict))

    def _token(self, eng, sem, v):
        clk = dict(eng.waited)
        clk[sem] = v
        self.clocks[(sem, v)] = (len(self.clocks), clk)
        return (sem, v)

    def _commit(self, tok, reads, writes):
        sem, v = tok
        for b in reads:
            if b.r.get(sem, 0) < v:
                b.r[sem] = v
        for b in writes:
            b.w = tok
            b.r = {}

    def op(self, en, fn, reads=(), writes=()):
        eng = self.engs[en]
        self._deps(eng, reads, writes)
        eng.sem.count += 1
        eng.prog.append(lambda h, fn=fn, s=eng.sem.h: fn(h).then_inc(s, 1))
        tok = self._token(eng, eng.sem, eng.sem.count)
        self._commit(tok, reads, writes)
        return tok

    def mm(self, mms, reads, out_buf):
        eng = self.engs["pe"]
        self._deps(eng, reads, [out_buf])
        eng.sem.count += 1

        def run(h, mms=mms, s=eng.sem.h):
            ins = None
            for (o, l, r, st, sp) in mms:
                ins = h.matmul(o, lhsT=l, rhs=r, start=st, stop=sp)
            ins.then_inc(s, 1)
        eng.prog.append(run)
        tok = self._token(eng, eng.sem, eng.sem.count)
        self._commit(tok, reads, [out_buf])
        return tok

    def dma(self, en, out_ap, in_ap, dsem, reads=(), writes=()):
        eng = self.engs[en]
        self._deps(eng, reads, writes, strict=True)
        dsem.count += 16
        eng.prog.append(lambda h, o=out_ap, i=in_ap, s=dsem.h: h.dma_start(out=o, in_=i).then_inc(s, 16))
        tok = self._token(eng, dsem, dsem.count)
        self._commit(tok, reads, writes)
        return tok

    def wait_tokens(self, en, toks):
        eng = self.engs[en]
        need = {}
        for sem, v in toks:
            if need.get(sem, 0) < v:
                need[sem] = v
        self._wait(eng, need)


class Pool:
    def __init__(self, items):
        self.free = list(items)

    def alloc(self):
        assert self.free, "temp pool exhausted"
        return self.free.pop(0)

    def release(self, it):
        self.free.append(it)


class T:
    def __init__(self, ap, name, sem=None):
        self.ap = ap
        self.b = Buf(name)
        self.sem = sem


class Builder:
    def __init__(self, layers=(0, 1), nseq=2, tps=SEQ // TT):
        self.layers = tuple(layers)
        self.nseq = nseq
        self.tps = tps
        self.nt = nseq * tps
        self.pieces = _piece_list(self.layers)
        self.np_pass = len(self.pieces)
        self.off, self.ncst = _cst_layout()

    def act(self, out, in_, func, reads, writes, bias=0.0, scale=1.0):
        return self.S.op("act", lambda e: e.activation(out=out, in_=in_, func=func, bias=bias, scale=scale), reads, writes)

    def tt(self, en, out, in0, in1, op, reads, writes):
        return self.S.op(en, lambda e: e.tensor_tensor(out=out, in0=in0, in1=in1, op=op), reads, writes)

    def stt(self, out, in0, scalar, in1, op0, op1, reads, writes):
        return self.S.op("dve", lambda e: e.scalar_tensor_tensor(out=out, in0=in0, scalar=scalar, in1=in1, op0=op0, op1=op1), reads, writes)

    def ts(self, en, out, in0, s1, s2, op0, op1, reads, writes):
        if s2 is None:
            return self.S.op(en, lambda e: e.tensor_scalar(out=out, in0=in0, scalar1=s1, scalar2=None, op0=op0), reads, writes)
        return self.S.op(en, lambda e: e.tensor_scalar(out=out, in0=in0, scalar1=s1, scalar2=s2, op0=op0, op1=op1), reads, writes)

    def copy(self, en, out, in_, reads, writes):
        return self.S.op(en, lambda e: e.tensor_copy(out=out, in_=in_), reads, writes)

    def memset(self, en, ap, val, writes):
        return self.S.op(en, lambda e: e.memset(ap, val), (), writes)

    def rsqrt(self, y_ap, y_b, x_ap, x_b, h_ap, h_b, t_ap, t_b, iters=2, h_on_act=False):
        I32 = mybir.dt.int32
        yi, xi = y_ap.bitcast(I32), x_ap.bitcast(I32)
        self.S.op("dve", lambda e: e.tensor_scalar(out=yi, in0=xi, scalar1=-0.5, scalar2=1597463007.0, op0=ALU.mult, op1=ALU.add), [x_b], [y_b])
        if h_on_act:
            self.act(h_ap, x_ap, AF.Identity, [x_b], [h_b], scale=-0.5)
        else:
            self.ts("dve", h_ap, x_ap, -0.5, None, ALU.mult, None, [x_b], [h_b])
        for _ in range(iters):
            self.tt("dve", t_ap, y_ap, y_ap, ALU.mult, [y_b], [t_b])
            self.tt("dve", t_ap, t_ap, h_ap, ALU.mult, [t_b, h_b], [t_b])
            self.stt(y_ap, t_ap, 1.5, y_ap, ALU.add, ALU.mult, [t_b, y_b], [y_b])

    def col(self, name, j):
        o = self.off[name] + j
        return self.cst[:, o:o + 1]

    def ws_issue(self, idx):
        if idx >= self.nt * self.np_pass:
            return
        s = idx % NSLOT
        name, ncols = self.pieces[idx % self.np_pass]
        src = self.wstream[idx % self.np_pass, :, 0:ncols]
        self.S.dma("pool", self.slot[s][:, 0:ncols], src, self.slot_sem[s], reads=(), writes=[self.slot_b[s]])

    def ws_acquire(self, expect):
        idx = self.ws_next
        name, ncols = self.pieces[idx % self.np_pass]
        assert name == expect, (name, expect)
        self.ws_next += 1
        s = idx % NSLOT
        return (idx, self.slot[s], self.slot_b[s])

    def ws_release(self, pc):
        self.ws_issue(pc[0] + NSLOT)

    def bank(self):
        i = self.bank_rr
        self.bank_rr = (self.bank_rr + 1) % 6
        return self.ps[i], self.ps_b[i]

    def build(self):
        nc = bass.Bass("TRN2", target_bir_lowering=False)
        self.nc = nc
        NT = self.nt
        x_d = nc.dram_tensor("x_dev", [NT, 128, 8 * TT], F32, kind="ExternalInput").ap()
        p_d = nc.dram_tensor("p_dev", [NT, 128, 4 * TT], F32, kind="ExternalInput").ap()
        self.wstream = nc.dram_tensor("wstream", [self.np_pass, 128, SLOTC], F32, kind="ExternalInput").ap()
        cst_d = nc.dram_tensor("cst", [128, self.ncst], F32, kind="ExternalInput").ap()
        y_d = nc.dram_tensor("y_dev", [NT, 128, 8 * TT], F32, kind="ExternalOutput").ap()

        with ExitStack() as es:
            E = es.enter_context
            S = Sched(nc, es)
            self.S = S
            sb = lambda name, shape, dt: E(nc.sbuf_tensor(name, shape, dt))
            self.cst = sb("cst_sb", [128, self.ncst], F32)
            cst_b = Buf("cst")
            x32 = sb("x32", [128, 8, TT], F32)
            xbfs = [sb(f"xbf{i}", [128, 8, TT], BF16) for i in range(2)]
            pbfs = [sb(f"pbf{i}", [128, 4, TT], BF16) for i in range(2)]
            arena = sb("arena", [128, 32 * 512], BF16)
            hbuf = sb("hbuf", [128, 4, 16 + TT], F32)
            cvbuf = sb("cvbuf", [128, 4, 2 + TT], F32)
            fhalo = sb("fhalo", [128, 2, NFF, 2], F32)
            wmT = sb("wmT", [128, 8, 128], BF16)
            Cc = sb("Cc", [128, 16, 128], F32)
            onesM = sb("onesM", [128, 128], BF16)
            ones = sb("ones", [128, 128], BF16)
            hbias = sb("hbias", [128, 16], F32)
            t32 = sb("t32", [128, NT32, TT], F32)
            za = sb("za", [128, 3, 2 + TT], F32)
            s528 = sb("s528", [128, 2, 16 + TT], F32)
            rsq = sb("rsq", [128, 3, TT], BF16)
            vraw = sb("vraw", [128, 2, 2048], F32)
            stt_ = sb("bnst", [128, 2, 4, 6], F32)
            mvt = sb("mvt", [128, 2, 8], F32)
            self.slot = [sb(f"slot{i}", [128, SLOTC], BF16) for i in range(NSLOT)]
            self.slot_b = [Buf(f"slot{i}") for i in range(NSLOT)]
            self.slot_sem = [S.new_sem(f"sem_slot{i}") for i in range(NSLOT)]
            self.ps = [E(nc.psum_tensor(f"ps{i}", [128, 512], F32)) for i in range(8)]
            self.ps_b = [Buf(f"ps{i}") for i in range(8)]
            self.bank_rr = 0
            sem_x32 = S.new_sem("sem_x32")
            sem_xbf = [S.new_sem(f"sem_xbf{i}") for i in range(2)]
            sem_pbf = [S.new_sem(f"sem_pbf{i}") for i in range(2)]
            sem_cst = S.new_sem("sem_cst")

            x32_b = [Buf(f"x32_{m}") for m in range(8)]
            xbf_bs = [[Buf(f"xbf{i}_{m}") for m in range(8)] for i in range(2)]
            pbf_bs = [Buf("pbf0"), Buf("pbf1")]
            xbf, xbf_b, pbf, pbf_b = xbfs[0], xbf_bs[0], pbfs[0], pbf_bs[0]
            ar_b = [Buf(f"ar{i}") for i in range(32)]
            hb_b = [Buf(f"hb{g}") for g in range(4)]
            cv_b = [Buf(f"cv{j}") for j in range(4)]
            fh_b = [[Buf(f"fh{l}_{c}") for c in range(NFF)] for l in range(2)]
            wm_b = Buf("wmT")
            cc_b = Buf("Cc")
            k_b = Buf("consts")
            tp = Pool([T(t32[:, i, :], f"t32_{i}", S.new_sem(f"sem_t{i}")) for i in range(NT32)])
            zap = Pool([(za[:, i, :], Buf(f"zah{i}"), Buf(f"zam{i}")) for i in range(3)])
            rsqp = Pool([T(rsq[:, i, :], f"rsq{i}") for i in range(3)])
            s5_b = [Buf("s528a"), Buf("s528b")]
            vr_b = [[Buf(f"vr{i}_{q}") for q in range(4)] for i in range(2)]
            st_b = [Buf("bnst0"), Buf("bnst1")]
            mv_b = [Buf("mv0"), Buf("mv1")]
            arv = lambda i: arena[:, i * 512:(i + 1) * 512]
            out_toks = []

            S.dma("sp", self.cst[:], cst_d, sem_cst, (), [cst_b])
            self.ws_next = 0
            for i in range(NSLOT):
                self.ws_issue(i)
            self.memset("dve", onesM[:], 1.0 / 1024.0, [k_b])
            self.memset("dve", ones[:], 1.0, [k_b])
            o = self.off
            for l in (0, 1):
                self.ts("dve", hbias[:, l * 8:(l + 1) * 8], self.cst[:, o[f"pb{l}"]:o[f"pb{l}"] + 8], 0.5, None, ALU.mult, None, [cst_b], [k_b])
            if 1 in self.layers:
                swv = self.cst[:, o["swT"]:o["swT"] + 1024].rearrange("p (h t) -> p h t", h=8)
                mk = self.cst[:, o["mask"]:o["mask"] + 128].unsqueeze(1).to_broadcast([128, 8, 128])
                self.tt("dve", wmT[:], swv, mk, ALU.mult, [cst_b], [wm_b])
                for hh in range(2):
                    bk, bkb = self.bank()
                    mms = [(bk[:, j * 128:(j + 1) * 128], ones[:], wmT[:, hh * 4 + j, :], True, True) for j in range(4)]
                    S.mm(mms, [k_b, wm_b], bkb)
                    for j in range(4):
                        h = hh * 4 + j
                        for e in range(2):
                            dc = 2 * h + e
                            self.stt(Cc[:, dc, :], bk[:, j * 128:(j + 1) * 128], self.col("sb", dc),
                                     self.cst[:, o["sbb"] + h * 128:o["sbb"] + (h + 1) * 128], ALU.mult, ALU.add,
                                     [bkb, cst_b], [cc_b])

            ln_pending = []

            def ln_flush():
                while ln_pending:
                    ln_pending.pop(0)()

            def ln_evac(m, bk, bkb):
                self.stt(x32[:, m, :], x32[:, m, :], ALPHA, bk[:, 0:TT], ALU.mult, ALU.add, [x32_b[m], bkb], [x32_b[m]])
                self.act(xbf[:, m, :], x32[:, m, :], AF.Identity, [x32_b[m]], [xbf_b[m]])
                rq = rsqp.alloc()
                self.act(rq.ap, x32[:, m, :], AF.Square, [x32_b[m]], [rq.b])
                ln_flush()

                def pend(m=m, rq=rq):
                    S.mm([(self.ps[6][:, 0:TT], onesM[:], xbf[:, m, :], m == 0, m == 7)], [k_b, xbf_b[m]], self.ps_b[6])
                    S.mm([(self.ps[7][:, 0:TT], onesM[:], rq.ap, m == 0, m == 7)], [k_b, rq.b], self.ps_b[7])
                    rsqp.release(rq)
                ln_pending.append(pend)

            def ln_norm(gname, bname):
                ln_flush()
                mean_sb, msq, ve, Aa, Bb = [tp.alloc() for _ in range(5)]
                b6, b7 = self.ps_b[6], self.ps_b[7]
                self.act(msq.ap, self.ps[6][:, 0:TT], AF.Square, [b6], [msq.b])
                self.act(mean_sb.ap, self.ps[6][:, 0:TT], AF.Identity, [b6], [mean_sb.b])
                self.stt(ve.ap, self.ps[7][:, 0:TT], EPS, msq.ap, ALU.add, ALU.subtract, [b7, msq.b], [ve.b])
                hh, tq = tp.alloc(), tp.alloc()
                c0 = tp.alloc()
                self.tt("dve", c0.ap, x32[:, 0, :], self.ps[6][:, 0:TT], ALU.subtract, [x32_b[0], b6], [c0.b])
                self.rsqrt(Aa.ap, Aa.b, ve.ap, ve.b, hh.ap, hh.b, tq.ap, tq.b, h_on_act=True)
                tp.release(hh)
                tp.release(tq)
                self.tt("dve", c0.ap, c0.ap, Aa.ap, ALU.mult, [c0.b, Aa.b], [c0.b])
                self.act(xbf[:, 0, :], c0.ap, AF.Identity, [c0.b, cst_b], [xbf_b[0]], bias=self.col(bname, 0), scale=self.col(gname, 0))
                self.act(x32[:, 0, :], c0.ap, AF.Identity, [c0.b, cst_b], [x32_b[0]], bias=self.col(bname, 0), scale=self.col(gname, 0))
                tp.release(c0)
                self.stt(Bb.ap, mean_sb.ap, -1.0, Aa.ap, ALU.mult, ALU.mult, [mean_sb.b, Aa.b], [Bb.b])
                for m0, nk in ((1, 2), (3, 2), (5, 2), (7, 1)):
                    ts_ = [tp.alloc() for _ in range(nk)]
                    for k in range(nk):
                        self.tt("dve", ts_[k].ap, x32[:, m0 + k, :], Aa.ap, ALU.mult, [x32_b[m0 + k], Aa.b], [ts_[k].b])
                    for k in range(nk):
                        self.tt("dve", ts_[k].ap, ts_[k].ap, Bb.ap, ALU.add, [ts_[k].b, Bb.b], [ts_[k].b])
                    for k in range(nk):
                        m = m0 + k
                        self.act(xbf[:, m, :], ts_[k].ap, AF.Identity, [ts_[k].b, cst_b], [xbf_b[m]], bias=self.col(bname, m), scale=self.col(gname, m))
                    for k in range(nk):
                        m = m0 + k
                        self.act(x32[:, m, :], ts_[k].ap, AF.Identity, [ts_[k].b, cst_b], [x32_b[m]], bias=self.col(bname, m), scale=self.col(gname, m))
                        tp.release(ts_[k])
                for t in (mean_sb, msq, ve, Aa, Bb):
                    tp.release(t)

            def ffn(l, li):
                fcw = lambda k, c: self.col(f"fcw{l}", k * NFF + c)
                up_tail = [None]
                for j in range(11):
                    pc = self.ws_acquire("up")
                    _, sl, slb = pc
                    if j == 0:
                        pre = [self.bank() for _ in range(4)]
                        for kc in range(8):
                            for i4 in range(4):
                                S.mm([(pre[i4][0][:, 0:TT], sl[:, kc * 512 + i4 * 128: kc * 512 + (i4 + 1) * 128], xbf[:, kc, :], kc == 0, kc == 7)],
                                     [slb, xbf_b[kc]], pre[i4][1])
                    for jj in range(2):
                        c = 2 * j + jj
                        if j == 0:
                            (bA, bAb), (bG, bGb) = pre[jj], pre[2 + jj]
                        else:
                            bA, bAb = self.bank()
                            bG, bGb = self.bank()
                            S.mm([(bA[:, 0:TT], sl[:, kc * 512 + jj * 128: kc * 512 + (jj + 1) * 128], xbf[:, kc, :], kc == 0, kc == 7) for kc in range(8)],
                                 [slb] + xbf_b, bAb)
                            S.mm([(bG[:, 0:TT], sl[:, kc * 512 + 256 + jj * 128: kc * 512 + 256 + (jj + 1) * 128], xbf[:, kc, :], kc == 0, kc == 7) for kc in range(8)],
                                 [slb] + xbf_b, bGb)
                        zt, zhb, zmb = zap.alloc()
                        fb = fh_b[li][c]
                        self.copy("pool", zt[:, 0:2], fhalo[:, li, c, :], [fb], [zhb])
                        self.act(zt[:, 2:2 + TT], bA[:, 0:TT], AF.Identity, [bAb], [zmb])
                        self.copy("pool", fhalo[:, li, c, :], zt[:, TT:TT + 2], [zmb], [fb])
                        t0 = tp.alloc()
                        self.act(t0.ap, zt[:, 0:TT], AF.Identity, [zhb, zmb, cst_b], [t0.b], bias=self.col(f"fcb{l}", c), scale=fcw(0, c))
                        self.stt(t0.ap, zt[:, 1:1 + TT], fcw(1, c), t0.ap, ALU.mult, ALU.add, [zhb, zmb, t0.b, cst_b], [t0.b])
                        self.stt(t0.ap, zt[:, 2:2 + TT], fcw(2, c), t0.ap, ALU.mult, ALU.add, [zmb, t0.b, cst_b], [t0.b])
                        zap.release((zt, zhb, zmb))
                        prev_tail = up_tail[0]

                        def tail(t0=t0, bG=bG, bGb=bGb, c=c):
                            self.act(t0.ap, t0.ap, AF.Gelu, [t0.b], [t0.b])
                            self.tt("dve", arv(c), t0.ap, bG[:, 0:TT], ALU.mult, [t0.b, bGb], [ar_b[c]])
                            tp.release(t0)
                        up_tail[0] = tail
                        if prev_tail is not None:
                            prev_tail()
                    self.ws_release(pc)
                up_tail[0]()
                up_tail[0] = None
                for m in range(8):
                    pc = self.ws_acquire("down")
                    _, sl, slb = pc
                    bk, bkb = self.bank()
                    if m == 0:
                        for kc in range(NFF):
                            S.mm([(bk[:, 0:TT], sl[:, kc * 128:(kc + 1) * 128], arv(kc), kc == 0, kc == NFF - 1)], [slb, ar_b[kc]], bkb)
                    else:
                        S.mm([(bk[:, 0:TT], sl[:, kc * 128:(kc + 1) * 128], arv(kc), kc == 0, kc == NFF - 1) for kc in range(NFF)],
                             [slb] + ar_b[0:NFF], bkb)
                    self.ws_release(pc)
                    ln_evac(m, bk, bkb)
                ln_norm(f"fg{l}", f"fb{l}")

            def ple(l, li, tile, last):
                g0 = self.ws_acquire("gate")
                pj = self.ws_acquire("proj")
                g1 = self.ws_acquire("gate")
                pre = [self.bank() for _ in range(4)]
                for kc in range(8):
                    for i4 in range(4):
                        S.mm([(pre[i4][0][:, 0:TT], g0[1][:, kc * 512 + i4 * 128: kc * 512 + (i4 + 1) * 128], xbf[:, kc, :], kc == 0, kc == 7)],
                             [g0[2], xbf_b[kc]], pre[i4][1])
                for m in range(8):
                    gp = g0 if m < 4 else g1
                    mmi = m % 4
                    if m < 4:
                        bT, bTb = pre[m]
                    else:
                        bT, bTb = self.bank()
                    bP, bPb = self.bank()
                    if m >= 4:
                        S.mm([(bT[:, 0:TT], gp[1][:, kc * 512 + mmi * 128: kc * 512 + (mmi + 1) * 128], xbf[:, kc, :], kc == 0, kc == 7) for kc in range(8)],
                             [gp[2]] + xbf_b, bTb)
                    S.mm([(bP[:, 0:TT], pj[1][:, kc * 1024 + m * 128: kc * 1024 + (m + 1) * 128], pbf[:, li * 2 + kc, :], kc == 0, kc == 1) for kc in range(2)],
                         [pj[2], pbf_b], bPb)
                    if m == 3:
                        self.ws_release(g0)
                    tg = tp.alloc()
                    g2 = tp.alloc()
                    self.act(tg.ap, bT[:, 0:TT], AF.Tanh, [bTb, k_b], [tg.b], bias=hbias[:, l * 8 + m: l * 8 + m + 1], scale=0.5)
                    self.stt(g2.ap, tg.ap, 1.0, bP[:, 0:TT], ALU.add, ALU.mult, [tg.b, bPb], [g2.b])
                    if last:
                        xo = tp.alloc()
                        self.stt(xo.ap, g2.ap, 0.5, x32[:, m, :], ALU.mult, ALU.add, [g2.b, x32_b[m]], [xo.b])
                        out_toks.append(S.dma("sp", y_d[tile, :, m * TT:(m + 1) * TT], xo.ap, xo.sem, reads=[xo.b], writes=()))
                        tp.release(xo)
                    else:
                        self.stt(x32[:, m, :], g2.ap, 0.5, x32[:, m, :], ALU.mult, ALU.add, [g2.b, x32_b[m]], [x32_b[m]])
                    tp.release(tg)
                    tp.release(g2)
                if not last:
                    for m in range(8):
                        self.act(xbf[:, m, :], x32[:, m, :], AF.Identity, [x32_b[m]], [xbf_b[m]])
                self.ws_release(pj)
                self.ws_release(g1)

            def mixer0(first):
                cw = lambda k, j: self.col("cw", k * 4 + j)
                pc = self.ws_acquire("in0")
                W = 16 + TT
                for g in range(4):
                    w = POOL_W[g]
                    bk, bkb = self.bank()
                    S.mm([(bk[:, 0:TT], pc[1][:, kc * 512 + g * 128: kc * 512 + (g + 1) * 128], xbf[:, kc, :], kc == 0, kc == 7) for kc in range(8)],
                         [pc[2]] + xbf_b, bkb)
                    self.act(hbuf[:, g, 16:W], bk[:, 0:TT], AF.Identity, [bkb], [hb_b[g]])
                    prev, prevb, lo, k, idx = hbuf[:, g, :], hb_b[g], 0, 1, 0
                    while k < w:
                        new, newb = s528[:, idx, :], s5_b[idx]
                        self.tt("dve", new[:, lo + k:W], prev[:, lo + k:W], prev[:, lo:W - k], ALU.add, [prevb], [newb])
                        prev, prevb, lo, k, idx = new, newb, lo + k, k * 2, 1 - idx
                    self.stt(arv(8 + g), prev[:, 16:W], 1.0 / w, hbuf[:, g, 16:W], ALU.mult, ALU.subtract, [prevb, hb_b[g]], [ar_b[8 + g]])
                    if first:
                        t = tp.alloc()
                        ic = self.cst[:, self.off["invcnt"] + g * 16: self.off["invcnt"] + (g + 1) * 16]
                        self.tt("dve", t.ap[:, 0:16], prev[:, 16:32], ic, ALU.mult, [prevb, cst_b], [t.b])
                        self.tt("dve", arv(8 + g)[:, 0:16], t.ap[:, 0:16], hbuf[:, g, 16:32], ALU.subtract, [t.b, hb_b[g]], [ar_b[8 + g]])
                        tp.release(t)
                    self.copy("pool", hbuf[:, g, 0:16], hbuf[:, g, TT:TT + 16], [hb_b[g]], [hb_b[g]])
                self.ws_release(pc)
                pc = self.ws_acquire("in0")
                cg = []
                for j in range(4):
                    bk, bkb = self.bank()
                    S.mm([(bk[:, 0:TT], pc[1][:, kc * 512 + j * 128: kc * 512 + (j + 1) * 128], xbf[:, kc, :], kc == 0, kc == 7) for kc in range(8)],
                         [pc[2]] + xbf_b, bkb)
                    t = tp.alloc()
                    self.act(t.ap, bk[:, 0:TT], AF.Identity, [bkb], [t.b])
                    cg.append(t)
                self.ws_release(pc)
                pc = self.ws_acquire("in0")
                cvt = []
                for j in range(4):
                    bk, bkb = self.bank()
                    S.mm([(bk[:, 0:TT], pc[1][:, kc * 512 + j * 128: kc * 512 + (j + 1) * 128], xbf[:, kc, :], kc == 0, kc == 7) for kc in range(8)],
                         [pc[2]] + xbf_b, bkb)
                    self.tt("dve", cvbuf[:, j, 2:2 + TT], cg[j].ap, bk[:, 0:TT], ALU.mult, [cg[j].b, bkb], [cv_b[j]])
                    tp.release(cg[j])
                    t = tp.alloc()
                    self.ts("dve", t.ap, cvbuf[:, j, 0:TT], cw(0, j), None, ALU.mult, None, [cv_b[j], cst_b], [t.b])
                    self.stt(t.ap, cvbuf[:, j, 1:1 + TT], cw(1, j), t.ap, ALU.mult, ALU.add, [cv_b[j], t.b, cst_b], [t.b])
                    self.stt(t.ap, cvbuf[:, j, 2:2 + TT], cw(2, j), t.ap, ALU.mult, ALU.add, [cv_b[j], t.b, cst_b], [t.b])
                    self.copy("pool", cvbuf[:, j, 0:2], cvbuf[:, j, TT:TT + 2], [cv_b[j]], [cv_b[j]])
                    cvt.append(t)
                self.ws_release(pc)
                pc = self.ws_acquire("in0")
                for j in range(4):
                    bk, bkb = self.bank()
                    S.mm([(bk[:, 0:TT], pc[1][:, kc * 512 + j * 128: kc * 512 + (j + 1) * 128], xbf[:, kc, :], kc == 0, kc == 7) for kc in range(8)],
                         [pc[2]] + xbf_b, bkb)
                    self.tt("dve", arv(4 + j), cvt[j].ap, bk[:, 0:TT], ALU.mult, [cvt[j].b, bkb], [ar_b[4 + j]])
                    tp.release(cvt[j])
                self.ws_release(pc)
                pc = self.ws_acquire("pmix")
                for g in range(4):
                    bk, bkb = self.bank()
                    S.mm([(bk[:, 0:TT], pc[1][:, g * 128:(g + 1) * 128], arv(8 + g), True, True)], [pc[2], ar_b[8 + g]], bkb)
                    self.act(arv(g), bk[:, 0:TT], AF.Identity, [bkb, cst_b], [ar_b[g]], scale=self.col("psc", g))
                self.ws_release(pc)
                for q in range(2):
                    pc = self.ws_acquire("out0")
                    for mmi in range(4):
                        m = q * 4 + mmi
                        bk, bkb = self.bank()
                        korder = (4, 5, 6, 7, 0, 1, 2, 3)
                        if m == 0:
                            for i, kc in enumerate(korder):
                                S.mm([(bk[:, 0:TT], pc[1][:, kc * 512 + mmi * 128: kc * 512 + (mmi + 1) * 128], arv(kc), i == 0, i == 7)],
                                     [pc[2], ar_b[kc]], bkb)
                        else:
                            S.mm([(bk[:, 0:TT], pc[1][:, kc * 512 + mmi * 128: kc * 512 + (mmi + 1) * 128], arv(kc), i == 0, i == 7) for i, kc in enumerate(korder)],
                                 [pc[2]] + ar_b[0:8], bkb)
                        ln_evac(m, bk, bkb)
                    self.ws_release(pc)
                ln_norm("mg0", "mb0")

            def mixer1():
                NCH = TT // 128
                vp = [self.ws_acquire("v") for _ in range(4)]
                pre = [self.bank() for _ in range(4)]
                for kc in range(8):
                    for q in range(4):
                        S.mm([(pre[q][0][:, 0:512], xbf[:, kc, 0:128], vp[q][1][:, kc * 512:(kc + 1) * 512], kc == 0, kc == 7)],
                             [vp[q][2], xbf_b[kc]], pre[q][1])
                ug_all = {}

                def u_groups(q4):
                    up = self.ws_acquire("u")
                    for mmi in range(4):
                        bU, bUb = self.bank()
                        S.mm([(bU[:, 0:TT], up[1][:, kc * 512 + mmi * 128: kc * 512 + (mmi + 1) * 128], xbf[:, kc, :], kc == 0, kc == 7) for kc in range(8)],
                             [up[2]] + xbf_b, bUb)
                        ug = tp.alloc()
                        self.act(ug.ap, bU[:, 0:TT], AF.Gelu, [bUb], [ug.b])
                        ug_all[q4 * 4 + mmi] = ug
                    self.ws_release(up)

                def s_groups(q4):
                    for mmi in range(4):
                        dc = q4 * 4 + mmi
                        h = dc // 2
                        bS, bSb = self.bank()
                        S.mm([(bS[:, c * 128:(c + 1) * 128], arena[:, c * 2048 + dc * 128: c * 2048 + (dc + 1) * 128], wmT[:, h, :], True, True) for c in range(NCH)],
                             [wm_b] + ar_b[0:4 * NCH], bSb)
                        mt = tp.alloc()
                        self.stt(mt.ap.rearrange("p (c t) -> p c t", c=NCH), bS[:, 0:TT].rearrange("p (c t) -> p c t", c=NCH), self.col("sg", dc),
                                 Cc[:, dc, :].unsqueeze(1).to_broadcast([128, NCH, 128]), ALU.mult, ALU.add, [bSb, cc_b, cst_b], [mt.b])
                        ug = ug_all.pop(dc)
                        self.tt("dve", arv(16 + dc), mt.ap, ug.ap, ALU.mult, [mt.b, ug.b], [ar_b[16 + dc]])
                        tp.release(ug)
                        tp.release(mt)

                for c in range(NCH):
                    vi = c % 2
                    if c == NCH - 1:
                        u_groups(0)
                    for q in range(4):
                        if c == 0:
                            bk, bkb = pre[q]
                        else:
                            bk, bkb = self.bank()
                            S.mm([(bk[:, 0:512], xbf[:, kc, c * 128:(c + 1) * 128], vp[q][1][:, kc * 512:(kc + 1) * 512], kc == 0, kc == 7) for kc in range(8)],
                                 [vp[q][2]] + xbf_b, bkb)
                        self.act(vraw[:, vi, q * 512:(q + 1) * 512], bk[:, 0:512], AF.Gelu, [bkb], [vr_b[vi][q]])
                        S.op("dve", lambda e, vi=vi, q=q: e.bn_stats(out=stt_[:, vi, q, :], in_=vraw[:, vi, q * 512:(q + 1) * 512]), [vr_b[vi][q]], [st_b[vi]])
                        if c == NCH - 1:
                            self.ws_release(vp[q])
                    S.op("dve", lambda e, vi=vi: e.bn_aggr(out=mvt[:, vi, 0:2], in_=stt_[:, vi, :, :]), [st_b[vi]], [mv_b[vi]])
                    self.ts("dve", mvt[:, vi, 2:3], mvt[:, vi, 1:2], EPS, None, ALU.add, None, [mv_b[vi]], [mv_b[vi]])
                    self.rsqrt(mvt[:, vi, 3:4], mv_b[vi], mvt[:, vi, 2:3], mv_b[vi], mvt[:, vi, 4:5], mv_b[vi], mvt[:, vi, 5:6], mv_b[vi])
                    self.ts("dve", arena[:, c * 2048:(c + 1) * 2048], vraw[:, vi, :], mvt[:, vi, 0:1], mvt[:, vi, 3:4], ALU.subtract, ALU.mult,
                            vr_b[vi] + [mv_b[vi]], ar_b[4 * c:4 * c + 4])
                u_groups(1)
                s_groups(0)
                s_groups(1)
                for q4 in (2, 3):
                    u_groups(q4)
                    s_groups(q4)
                for q in range(4):
                    pc = self.ws_acquire("out1")
                    for mmi in range(2):
                        m = q * 2 + mmi
                        bk, bkb = self.bank()
                        S.mm([(bk[:, 0:TT], pc[1][:, kc * 256 + mmi * 128: kc * 256 + (mmi + 1) * 128], arv(16 + kc), kc == 0, kc == 15) for kc in range(16)],
                             [pc[2]] + ar_b[16:32], bkb)
                        ln_evac(m, bk, bkb)
                    self.ws_release(pc)
                ln_norm("mg1", "mb1")

            def load_bf(t):
                i = t % 2
                S.dma("pool", xbfs[i][:].rearrange("p c t -> p (c t)"), x_d[t], sem_xbf[i], (), xbf_bs[i])
                S.dma("pool", pbfs[i][:].rearrange("p c t -> p (c t)"), p_d[t], sem_pbf[i], (), [pbf_bs[i]])

            load_bf(0)
            for tile in range(NT):
                first = (tile % self.tps) == 0
                xbf, xbf_b, pbf, pbf_b = xbfs[tile % 2], xbf_bs[tile % 2], pbfs[tile % 2], pbf_bs[tile % 2]
                S.dma("sp", x32[:].rearrange("p c t -> p (c t)"), x_d[tile], sem_x32, (), x32_b)
                if tile + 1 < NT:
                    load_bf(tile + 1)
                if first:
                    self.memset("pool", hbuf[:, :, 0:16], 0.0, hb_b)
                    self.memset("pool", cvbuf[:, :, 0:2], 0.0, cv_b)
                    self.memset("pool", fhalo[:].rearrange("p a b c -> p (a b c)"), 0.0, fh_b[0] + fh_b[1])
                for li, l in enumerate(self.layers):
                    if l == 0:
                        mixer0(first)
                    else:
                        mixer1()
                    ffn(l, li)
                    ple(l, li, tile, li == len(self.layers) - 1)
            assert self.ws_next == NT * self.np_pass
            self.sbuf_left = nc.sbuf_bytes_remaining
            S.wait_tokens("sp", out_toks)

            with nc.Block() as block:
                @block.tensor
                def _(h):
                    for c in S.engs["pe"].prog:
                        c(h)

                @block.scalar
                def _(h):
                    for c in S.engs["act"].prog:
                        c(h)

                @block.vector
                def _(h):
                    for c in S.engs["dve"].prog:
                        c(h)

                @block.gpsimd
                def _(h):
                    for c in S.engs["pool"].prog:
                        c(h)

                @block.sync
                def _(h):
                    for c in S.engs["sp"].prog:
                        c(h)
        return nc


def _x_to_dev(xs):
    n, S_, _ = xs.shape
    a = xs.reshape(n, S_ // TT, TT, 8, 128).transpose(0, 1, 4, 3, 2)
    return np.ascontiguousarray(a).reshape(n * (S_ // TT), 128, 8 * TT)


def _p_to_dev(ps):
    L, n, S_, _ = ps.shape
    a = ps.reshape(L, n, S_ // TT, TT, 2, 128).transpose(1, 2, 5, 0, 4, 3)
    return np.ascontiguousarray(a).reshape(n * (S_ // TT), 128, L * 2 * TT)


def _y_from_dev(y, n, S_):
    a = y.reshape(n, S_ // TT, 128, 8, TT).transpose(0, 1, 4, 3, 2)
    return np.ascontiguousarray(a).reshape(n, S_, 1024)


def run(inputs, layers=(0, 1), ncores=NCORES, nseq=2, seqlen=SEQ):
    inp = {k: np.asarray(v) for k, v in inputs.items()}
    x = inp["x"].astype(np.float32, copy=False)
    p = inp["p"].astype(np.float32, copy=False)
    b = Builder(layers=layers, nseq=nseq, tps=seqlen // TT)
    nc = b.build()
    ws = _build_wstream(inp, layers)
    cst = _build_cst(inp)
    psel = p[list(layers)]
    if len(layers) == 1:
        psel = np.concatenate([psel, np.zeros_like(psel)], 0)
    in_maps = []
    for c in range(ncores):
        sl = slice(c * nseq, (c + 1) * nseq)
        in_maps.append({
            "x_dev": _x_to_dev(x[sl, :seqlen]),
            "p_dev": _p_to_dev(psel[:, sl, :seqlen]),
            "wstream": ws,
            "cst": cst,
        })
    res = run_bass_kernel_spmd(nc, in_maps, core_ids=list(range(ncores)))
    outs = [_y_from_dev(np.asarray(r["y_dev"]), nseq, seqlen) for r in res.results]
    return np.concatenate(outs, 0).astype(np.float32)


def kernel(**inputs):
    return run(inputs)
```
